# Optimizing a Trainium2 kernel written in Bass

```python
import jax, jax.numpy as jnp
from jax import lax
import numpy as np

D_MODEL = 1024
BATCH = 2
SEQ = 16384
DEPTH = 4

N_HEADS = 8
HEAD_DIM = 128
D_ATTN = N_HEADS * HEAD_DIM
Q_BLOCK = 128
FORGET_BIAS = 2.0
D_RNN = D_MODEL
N_RNN_BLOCKS = 16
RNN_BLOCK = D_RNN // N_RNN_BLOCKS
CONV_WIDTH = 4
LRU_C = 8.0
D_FF = 2816
NORM_EPS = 1e-6

SPLIT_POINTS = (
    D_ATTN,
    2 * D_ATTN,
    3 * D_ATTN,
    3 * D_ATTN + N_HEADS,
    3 * D_ATTN + N_HEADS + D_RNN,
    3 * D_ATTN + N_HEADS + 2 * D_RNN,
    3 * D_ATTN + N_HEADS + 2 * D_RNN + D_MODEL,
)
N_IN = 3 * D_ATTN + N_HEADS + 2 * D_RNN + 2 * D_MODEL

kernel_name = "fox_rglru_macaron_hybrid"


def rms_norm(x, g):
    xf = x.astype(jnp.float32)
    y = xf * lax.rsqrt(jnp.mean(xf * xf, axis=-1, keepdims=True) + NORM_EPS)
    return (y * g.astype(jnp.float32)).astype(x.dtype)


def swiglu(x, w_gate, w_up, w_down):
    return (jax.nn.silu(x @ w_gate) * (x @ w_up)) @ w_down


def forgetting_attention(q, k, v, log_f):
    B, S = q.shape[0], q.shape[1]
    nb = S // Q_BLOCK
    scale = HEAD_DIM ** -0.5
    c = jnp.cumsum(log_f, axis=1).transpose(0, 2, 1)
    q = q.transpose(0, 2, 1, 3)
    k = k.transpose(0, 2, 1, 3)
    v = v.transpose(0, 2, 1, 3)
    qb = q.reshape(B, N_HEADS, nb, Q_BLOCK, HEAD_DIM).transpose(2, 0, 1, 3, 4)
    cb = c.reshape(B, N_HEADS, nb, Q_BLOCK).transpose(2, 0, 1, 3)
    k_pos = jnp.arange(S)

    def one_block(args):
        q_i, c_i, i = args
        s = jnp.einsum('bhqd,bhkd->bhqk', q_i, k).astype(jnp.float32) * scale
        s = s + (c_i[..., :, None] - c[:, :, None, :])
        q_pos = i * Q_BLOCK + jnp.arange(Q_BLOCK)
        mask = k_pos[None, :] <= q_pos[:, None]
        s = jnp.where(mask, s, -jnp.inf)
        p = jax.nn.softmax(s, axis=-1)
        return jnp.einsum('bhqk,bhkd->bhqd', p.astype(v.dtype), v)

    out = lax.map(one_block, (qb, cb, jnp.arange(nb)))
    return out.transpose(1, 0, 3, 2, 4).reshape(B, S, D_ATTN)


def causal_depthwise_conv(x, w, b):
    S = x.shape[1]
    xp = jnp.pad(x, ((0, 0), (CONV_WIDTH - 1, 0), (0, 0)))
    y = b
    for tap in range(CONV_WIDTH):
        y = y + xp[:, tap:tap + S, :] * w[tap]
    return y


def block_diag_linear(x, w, b):
    B, S = x.shape[0], x.shape[1]
    xb = x.reshape(B, S, N_RNN_BLOCKS, RNN_BLOCK)
    return jnp.einsum('bsnc,ncd->bsnd', xb, w).reshape(B, S, D_RNN) + b


def rg_lru(x, w_a, b_a, w_x, b_x, lam):
    xf = x.astype(jnp.float32)
    r = jax.nn.sigmoid(block_diag_linear(x, w_a, b_a).astype(jnp.float32))
    i = jax.nn.sigmoid(block_diag_linear(x, w_x, b_x).astype(jnp.float32))
    log_a = -LRU_C * r * jax.nn.softplus(-lam.astype(jnp.float32))
    a = jnp.exp(log_a)
    u = jnp.sqrt(-jnp.expm1(2.0 * log_a)) * (i * xf)

    def combine(left, right):
        a1, b1 = left
        a2, b2 = right
        return a1 * a2, a2 * b1 + b2

    _, h = lax.associative_scan(combine, (a, u), axis=1)
    return h.astype(x.dtype)


def setup_inputs(seed: int = 0) -> dict:
    key = jax.random.key(seed)
    ks = jax.random.split(key, 24)
    L, D, F = DEPTH, D_MODEL, D_FF
    nrm = lambda k, shape, fan_in: jax.random.normal(k, shape, jnp.float32) * (fan_in ** -0.5)
    gain = lambda k, shape: 1.0 + 0.02 * jax.random.normal(k, shape, jnp.float32)
    bias = lambda k, shape: 0.02 * jax.random.normal(k, shape, jnp.float32)

    x = jax.random.normal(ks[0], (BATCH, SEQ, D), jnp.float32)
    b_in = bias(ks[6], (L, N_IN))
    b_in = b_in.at[:, SPLIT_POINTS[2]:SPLIT_POINTS[3]].add(FORGET_BIAS)
    u = jax.random.uniform(ks[15], (L, D_RNN), jnp.float32, 0.9, 0.999)
    lru_lambda = jnp.log(u) - jnp.log1p(-u)
    return {
        "x": x,
        "ffn1_norm": gain(ks[1], (L, D)),
        "ffn1_w_gate": nrm(ks[2], (L, D, F), D),
        "ffn1_w_up": nrm(ks[3], (L, D, F), D),
        "ffn1_w_down": nrm(ks[4], (L, F, D), F),
        "mix_norm": gain(ks[5], (L, D)),
        "w_in": nrm(ks[7], (L, D, N_IN), D),
        "b_in": b_in,
        "q_norm": gain(ks[8], (L, HEAD_DIM)),
        "k_norm": gain(ks[9], (L, HEAD_DIM)),
        "conv_w": nrm(ks[10], (L, CONV_WIDTH, D_RNN), CONV_WIDTH),
        "conv_b": bias(ks[11], (L, D_RNN)),
        "lru_w_a": nrm(ks[12], (L, N_RNN_BLOCKS, RNN_BLOCK, RNN_BLOCK), RNN_BLOCK),
        "lru_b_a": bias(ks[13], (L, D_RNN)),
        "lru_w_x": nrm(ks[14], (L, N_RNN_BLOCKS, RNN_BLOCK, RNN_BLOCK), RNN_BLOCK),
        "lru_b_x": bias(ks[16], (L, D_RNN)),
        "lru_lambda": lru_lambda,
        "w_o_attn": nrm(ks[17], (L, D_ATTN, D), D_ATTN),
        "w_o_rnn": nrm(ks[18], (L, D_RNN, D), D_RNN),
        "w_out": nrm(ks[19], (L, D, D), D),
        "ffn2_norm": gain(ks[20], (L, D)),
        "ffn2_w_gate": nrm(ks[21], (L, D, F), D),
        "ffn2_w_up": nrm(ks[22], (L, D, F), D),
        "ffn2_w_down": nrm(ks[23], (L, F, D), F),
    }


def reference(x, ffn1_norm, ffn1_w_gate, ffn1_w_up, ffn1_w_down, mix_norm, w_in, b_in,
              q_norm, k_norm, conv_w, conv_b, lru_w_a, lru_b_a, lru_w_x, lru_b_x, lru_lambda,
              w_o_attn, w_o_rnn, w_out, ffn2_norm, ffn2_w_gate, ffn2_w_up, ffn2_w_down):
    B, S = x.shape[0], x.shape[1]
    for l in range(DEPTH):
        x = x + 0.5 * swiglu(rms_norm(x, ffn1_norm[l]), ffn1_w_gate[l], ffn1_w_up[l], ffn1_w_down[l])

        h = rms_norm(x, mix_norm[l])
        proj = h @ w_in[l] + b_in[l]
        q, k, v, f_logit, xr, gr, g_attn, g_rnn = jnp.split(proj, SPLIT_POINTS, axis=-1)

        q = rms_norm(q.reshape(B, S, N_HEADS, HEAD_DIM), q_norm[l])
        k = rms_norm(k.reshape(B, S, N_HEADS, HEAD_DIM), k_norm[l])
        v = v.reshape(B, S, N_HEADS, HEAD_DIM)
        log_f = jax.nn.log_sigmoid(f_logit.astype(jnp.float32))
        y_attn = forgetting_attention(q, k, v, log_f) @ w_o_attn[l]

        xr = causal_depthwise_conv(xr, conv_w[l], conv_b[l])
        yr = rg_lru(xr, lru_w_a[l], lru_b_a[l], lru_w_x[l], lru_b_x[l], lru_lambda[l]) * jax.nn.gelu(gr)
        y_rnn = yr @ w_o_rnn[l]

        merged = jax.nn.sigmoid(g_attn) * y_attn + jax.nn.sigmoid(g_rnn) * y_rnn
        x = x + merged @ w_out[l]

        x = x + 0.5 * swiglu(rms_norm(x, ffn2_norm[l]), ffn2_w_gate[l], ffn2_w_up[l], ffn2_w_down[l])
    return x
```

```python
import contextlib
import numpy as np
import ml_dtypes
import concourse.bass as bass
import concourse.mybir as mybir
from concourse.bass_utils import run_bass_kernel_spmd

F32 = mybir.dt.float32
BF16 = mybir.dt.bfloat16
ALU = mybir.AluOpType
AF = mybir.ActivationFunctionType

D = 1024
DFF = 2816
NH = 8
HD = 128
SEQ = 16384
NB = 2
DEPTH = 4
NCORE = 8
EPS = 1e-6
NFC = DFF // 128
NKC = D // 128
NPROJ = 48


class Buf:
    __slots__ = ("name", "last_w", "readers")

    def __init__(self, name):
        self.name = name
        self.last_w = None
        self.readers = []


class Prog:
    ENGS = ("pe", "act", "dve", "pool", "sp")
    K = 12

    def __init__(self, nc):
        self.nc = nc
        self.stack = contextlib.ExitStack()
        self.streams = {e: [] for e in self.ENGS}
        self.count = {e: 0 for e in self.ENGS}
        self.seen = {e: {} for e in self.ENGS}
        self.dma_n = {e: 0 for e in self.ENGS}
        self.nbuf = 0

    def buf(self, name=None):
        self.nbuf += 1
        return Buf(name or f"b{self.nbuf}")

    def sbuf(self, name, shape, dt):
        return self.stack.enter_context(self.nc.sbuf_tensor(name, list(shape), dt))

    def psum(self, name, shape, dt=F32):
        return self.stack.enter_context(self.nc.psum_tensor(name, list(shape), dt))

    def _need(self, eng, tok, raw):
        if tok is None:
            return
        kind, key, val = tok
        if kind == "e" and key == eng:
            if eng in ("pe", "sp") or not raw:
                return
        cur = self.seen[eng].get((kind, key), 0)
        if val > cur:
            self.seen[eng][(kind, key)] = val
            self.streams[eng].append(("wait", (kind, key), val))

    def _deps(self, eng, reads, writes):
        for r in reads:
            self._need(eng, r.last_w, True)
        for w in writes:
            self._need(eng, w.last_w, False)
            for t in w.readers:
                self._need(eng, t, False)

    def _commit(self, tok, reads, writes):
        for r in reads:
            if tok[0] == "e":
                r.readers = [t for t in r.readers if not (t[0] == "e" and t[1] == tok[1])]
            r.readers.append(tok)
        for w in writes:
            w.last_w = tok
            w.readers = []

    def op(self, eng, fn, reads=(), writes=()):
        self._deps(eng, reads, writes)
        self.count[eng] += 1
        tok = ("e", eng, self.count[eng])
        self.streams[eng].append(("op", fn))
        self._commit(tok, reads, writes)
        return tok

    def dma(self, q, fn, reads=(), writes=()):
        self._deps(q, reads, writes)
        i = self.dma_n[q]
        self.dma_n[q] += 1
        slot = i % self.K
        gen = i // self.K
        key = (q, slot)
        if gen > 0:
            self._need(q, ("d", key, 16 * gen), True)
        tok = ("d", key, 16 * (gen + 1))
        self.streams[q].append(("dma", fn, key))
        self._commit(tok, reads, writes)
        return tok

    def emit(self):
        nc = self.nc
        st = self.stack
        sems = {}
        for e in self.ENGS:
            if self.count[e]:
                sems[("e", e)] = st.enter_context(nc.semaphore(f"s_{e}"))
        for q in self.ENGS:
            for s in range(min(self.K, self.dma_n[q])):
                sems[("d", (q, s))] = st.enter_context(nc.semaphore(f"d_{q}{s}"))
        for q in self.ENGS:
            n = self.dma_n[q]
            for s in range(min(self.K, n)):
                gens = (n - 1 - s) // self.K + 1
                self._need("sp", ("d", (q, s), 16 * gens), True)
        for e in self.ENGS:
            if e != "sp" and self.count[e]:
                self._need("sp", ("e", e, self.count[e]), True)
        block = st.enter_context(nc.Block())
        engmap = {"pe": block.tensor, "act": block.scalar, "dve": block.vector,
                  "pool": block.gpsimd, "sp": block.sync}

        def run(e):
            def body(eng):
                own = sems.get(("e", e))
                for item in self.streams[e]:
                    if item[0] == "wait":
                        eng.wait_ge(sems[item[1]], item[2])
                    elif item[0] == "op":
                        item[1](eng).then_inc(own, 1)
                    else:
                        item[1](eng).then_inc(sems[("d", item[2])], 16)
            return body

        for e in self.ENGS:
            if self.streams[e]:
                engmap[e](run(e))
        st.close()


class Ring:
    def __init__(self, P, name, n, shape, dt, psum=False):
        mk = P.psum if psum else P.sbuf
        self.tiles = [mk(f"{name}{i}", shape, dt) for i in range(n)]
        self.bufs = [P.buf(f"{name}{i}") for i in range(n)]
        self.i = 0

    def next(self):
        k = self.i % len(self.tiles)
        self.i += 1
        return self.tiles[k], self.bufs[k]


def MM(P, out, lhsT, rhs, start, stop, reads, writes):
    P.op("pe", lambda e: e.matmul(out, lhsT=lhsT, rhs=rhs, start=start, stop=stop), reads, writes)


def ACT(P, out, in_, func, reads, writes, bias=None, scale=None):
    kw = {}
    if bias is not None:
        kw["bias"] = bias
    if scale is not None:
        kw["scale"] = scale
    P.op("act", lambda e: e.activation(out=out, in_=in_, func=func, **kw), reads, writes)


def TS(P, eng, out, in0, s1, s2, op0, op1, reads, writes):
    if s2 is None:
        P.op(eng, lambda e: e.tensor_scalar(out=out, in0=in0, scalar1=s1, scalar2=None, op0=op0), reads, writes)
    else:
        P.op(eng, lambda e: e.tensor_scalar(out=out, in0=in0, scalar1=s1, scalar2=s2, op0=op0, op1=op1), reads, writes)


def STT(P, eng, out, in0, scalar, in1, op0, op1, reads, writes):
    P.op(eng, lambda e: e.scalar_tensor_tensor(out=out, in0=in0, scalar=scalar, in1=in1, op0=op0, op1=op1), reads, writes)


def TT(P, eng, out, in0, in1, op, reads, writes):
    P.op(eng, lambda e: e.tensor_tensor(out=out, in0=in0, in1=in1, op=op), reads, writes)


def DMA(P, q, out, in_, reads, writes):
    P.dma(q, lambda e: e.dma_start(out=out, in_=in_), reads, writes)


def build_tok(ntok, do_c, do_a, T=512):
    nc = bass.Bass("TRN2", target_bir_lowering=False)
    P = Prog(nc)
    ntile = ntok // T

    def din(name, shape, dt=F32):
        return nc.dram_tensor(name, list(shape), dt, kind="ExternalInput").ap()

    def dout(name, shape, dt=F32):
        return nc.dram_tensor(name, list(shape), dt, kind="ExternalOutput").ap()

    def dscr(name, shape, dt=BF16):
        return nc.dram_tensor(name, list(shape), dt, kind="Internal").ap()

    xT = din("xT", [NKC, 128, ntok])
    xo = dout("xo", [NKC, 128, ntok])

    wspecs = []

    def wdecl(name, rows, cols=1024):
        a = din(name, [rows, cols])
        s = dscr(name + "_s", [rows, cols])
        b = P.buf(name)
        wspecs.append((a, s, b, rows))
        return s, b

    ffn = {}

    def ffn_decl(tag):
        ffn[tag] = dict(
            wg=wdecl(tag + "_wg", NFC * 128), wu=wdecl(tag + "_wu", NFC * 128),
            wd=wdecl(tag + "_wd", NKC * 128 * 2, NFC * 64),
            g=din(tag + "_g", [128, NKC]))

    if do_c:
        woa_s, woa_b = wdecl("woa", NKC * 128)
        wor_s, wor_b = wdecl("wor", NKC * 128)
        wout_s, wout_b = wdecl("wout", NKC * 128)
        ffn_decl("f2")
        attnT = din("attnT", [NKC, 128, ntok], BF16)
        yrT = din("yrT", [NKC, 128, ntok], BF16)
        gaT = din("gaT", [NKC, 128, ntok])
        grT = din("grT", [NKC, 128, ntok])
    if do_a:
        ffn_decl("f1")
        win_s, win_b = wdecl("win", NPROJ * 128)
        wv_s, wv_b = wdecl("wv", 1024)
        wf_in = din("wf", [128, NKC * 8])
        mixg = din("mixg", [128, NKC])
        bfm = din("bfm", [128, NPROJ])
        bv = din("bv", [1, 1024])
        bfl = din("bfl", [1, 8])
        qg = din("qg", [128, 1])
        kg = din("kg", [128, 1])
        qTo = dout("qTo", [NH, 128, ntok], BF16)
        kTo = dout("kTo", [NH, 128, ntok], BF16)
        vo = dout("vo", [ntok, 1024], BF16)
        lfo = dout("lfo", [ntok, 8])
        xrTo = dout("xrTo", [NKC, 128, ntok])
        grTo = dout("grTo", [NKC, 128, ntok])
        gaTo = dout("gaTo", [NKC, 128, ntok])
        ggTo = dout("ggTo", [NKC, 128, ntok])

    for (a, s, b, rows) in wspecs:
        step = 2048
        for r0 in range(0, rows, step):
            r1 = min(rows, r0 + step)
            DMA(P, "pool", s[r0:r1, :], a[r0:r1, :], [], [b])

    ones_bf = P.sbuf("ones_bf", [128, 128], BF16)
    Bconst = P.buf("const")
    P.op("dve", lambda e: e.memset(ones_bf[:], 1.0), [], [Bconst])

    eps_sb = P.sbuf("eps_sb", [128, 1], F32)
    P.op("dve", lambda e: e.memset(eps_sb[:], EPS), [], [Bconst])
    one_sb = P.sbuf("one_sb", [128, 1], F32)
    P.op("dve", lambda e: e.memset(one_sb[:], 1.0), [], [Bconst])

    def small(name, src, shape):
        t = P.sbuf(name, shape, F32)
        DMA(P, "sp", t[:], src, [], [Bconst])
        return t

    for tag in ffn:
        ffn[tag]["g_sb"] = small(tag + "_gsb", ffn[tag]["g"], [128, NKC])
    if do_a:
        mixg_sb = small("mixg_sb", mixg, [128, NKC])
        bfm_sb = small("bfm_sb", bfm, [128, NPROJ])
        qg_sb = small("qg_sb", qg, [128, 1])
        kg_sb = small("kg_sb", kg, [128, 1])
        bv_sb = small("bv_sb", bv.to_broadcast([128, 1024]), [128, 1024])
        bfl_sb = small("bfl_sb", bfl.to_broadcast([128, 8]), [128, 8])
        wf_f = small("wf_f", wf_in, [128, NKC * 8])
        wf_b = P.sbuf("wf_b", [128, NKC * 8], BF16)
        P.op("dve", lambda e: e.tensor_copy(out=wf_b[:], in_=wf_f[:]), [Bconst], [Bconst])
        P.op("dve", lambda e: e.tensor_scalar(out=qg_sb[:], in0=qg_sb[:], scalar1=float(HD ** -0.5),
                                              scalar2=None, op0=ALU.mult), [Bconst], [Bconst])

    x_sb = P.sbuf("x_sb", [128, NKC, T], F32); Bx = P.buf("x")
    xn = P.sbuf("xn", [128, NKC, T], BF16); Bxn = P.buf("xn")
    sq = P.sbuf("sq", [128, NKC, T], BF16); Bsq = P.buf("sq")
    hT = P.sbuf("hT", [128, NFC, T], BF16); BhT = P.buf("hT")
    sd_r = Ring(P, "sd", 2, [128, T], F32)
    rs_r = Ring(P, "rs", 2, [128, T], F32)
    sg_r = Ring(P, "sg", 3, [128, T], F32)
    ps_r = Ring(P, "ps", 8, [128, 512], F32, psum=True)
    w1_r = Ring(P, "w1", 6, [128, 1024], BF16)
    wd_r = Ring(P, "wdr", 2, [128, NFC * 128], BF16)
    ev_r = Ring(P, "ev", 4, [128, T], F32)
    if do_a:
        evb_r = Ring(P, "evb", 3, [128, T], BF16)
        wv_sb = P.sbuf("wv_sb", [128, 2, NKC * 512], BF16); Bwvsb = P.buf("wvsb")
        vb_r = Ring(P, "vb", 2, [128, 512], BF16)
        f_r = Ring(P, "fr", 2, [128, 4, 8], F32)
        f2_r = Ring(P, "fr2", 2, [128, 4, 8], F32)
    if do_c:
        at_sb = P.sbuf("at_sb", [128, NKC, T], BF16); Bat = P.buf("at")
        yr_sb = P.sbuf("yr_sb", [128, NKC, T], BF16); Byr = P.buf("yr")
        ga_sb = P.sbuf("ga_sb", [128, NKC, T], F32); Bga = P.buf("ga")
        gr_sb = P.sbuf("gr_sb", [128, NKC, T], F32); Bgr = P.buf("gr")
        m1 = P.sbuf("m1", [128, NKC, T], F32); Bm1 = P.buf("m1")
        mg = P.sbuf("mg", [128, NKC, T], BF16); Bmg = P.buf("mg")

    def rmsnorm(g_sb):
        for kc in range(NKC):
            ACT(P, sq[:, kc, :], x_sb[:, kc, :], AF.Square, [Bx], [Bsq])
        pn, Bpn = ps_r.next()
        for kc in range(NKC):
            MM(P, pn[:, 0:T], ones_bf[:], sq[:, kc, :], kc == 0, kc == NKC - 1, [Bconst, Bsq], [Bpn])
        sd, Bsd = sd_r.next()
        ACT(P, sd[:], pn[:, 0:T], AF.Sqrt, [Bpn], [Bsd], bias=eps_sb[:, 0:1], scale=1.0 / D)
        rs, Brs = rs_r.next()
        P.op("dve", lambda e: e.reciprocal(out=rs[:], in_=sd[:]), [Bsd], [Brs])
        for kc in range(NKC):
            STT(P, "dve", xn[:, kc, :], x_sb[:, kc, :], g_sb[:, kc:kc + 1], rs[:], ALU.mult, ALU.mult,
                [Bx, Brs, Bconst], [Bxn])

    def ffn_block(tag):
        f = ffn[tag]
        wg_s, wg_b = f["wg"]
        wu_s, wu_b = f["wu"]
        wdn_s, wdn_b = f["wd"]
        rmsnorm(f["g_sb"])
        for j in range(NFC):
            wgb, Bwg = w1_r.next()
            DMA(P, "sp", wgb[:], wg_s[j * 128:(j + 1) * 128, :], [wg_b], [Bwg])
            wub, Bwu = w1_r.next()
            DMA(P, "sp", wub[:], wu_s[j * 128:(j + 1) * 128, :], [wu_b], [Bwu])
            pg, Bpg = ps_r.next()
            for kc in range(NKC):
                MM(P, pg[:, 0:T], wgb[:, kc * 128:(kc + 1) * 128], xn[:, kc, :], kc == 0, kc == NKC - 1,
                   [Bwg, Bxn], [Bpg])
            pu, Bpu = ps_r.next()
            for kc in range(NKC):
                MM(P, pu[:, 0:T], wub[:, kc * 128:(kc + 1) * 128], xn[:, kc, :], kc == 0, kc == NKC - 1,
                   [Bwu, Bxn], [Bpu])
            sg, Bsg = sg_r.next()
            ACT(P, sg[:], pg[:, 0:T], AF.Silu, [Bpg], [Bsg])
            TT(P, "dve", hT[:, j, :], sg[:], pu[:, 0:T], ALU.mult, [Bsg, Bpu], [BhT])
        wdn3 = wdn_s.rearrange("(dc p r) c -> dc p (r c)", dc=NKC, p=128, r=2)
        for dc in range(NKC):
            wdb, Bwd = wd_r.next()
            DMA(P, "sp", wdb[:], wdn3[dc], [wdn_b], [Bwd])
            po, Bpo = ps_r.next()
            for fc in range(NFC):
                MM(P, po[:, 0:T], wdb[:, fc * 128:(fc + 1) * 128], hT[:, fc, :], fc == 0, fc == NFC - 1,
                   [Bwd, BhT], [Bpo])
            STT(P, "dve", x_sb[:, dc, :], po[:, 0:T], 0.5, x_sb[:, dc, :], ALU.mult, ALU.add, [Bpo, Bx], [Bx])


    def proj_fm(w_s, w_b, j, rhs_t, Brhs):
        wb, Bw = w1_r.next()
        DMA(P, "sp", wb[:], w_s[j * 128:(j + 1) * 128, :], [w_b], [Bw])
        pp, Bpp = ps_r.next()
        for kc in range(NKC):
            MM(P, pp[:, 0:T], wb[:, kc * 128:(kc + 1) * 128], rhs_t[:, kc, :], kc == 0, kc == NKC - 1,
               [Bw, Brhs], [Bpp])
        return pp, Bpp

    for it in range(ntile):
        t0 = it * T
        xT_t = xT.rearrange("kc p t -> p kc t")
        DMA(P, "sp", x_sb[:], xT_t[:, :, t0:t0 + T], [], [Bx])
        if do_c:
            DMA(P, "sp", at_sb[:], attnT.rearrange("kc p t -> p kc t")[:, :, t0:t0 + T], [], [Bat])
            DMA(P, "sp", yr_sb[:], yrT.rearrange("kc p t -> p kc t")[:, :, t0:t0 + T], [], [Byr])
            DMA(P, "sp", ga_sb[:], gaT.rearrange("kc p t -> p kc t")[:, :, t0:t0 + T], [], [Bga])
            DMA(P, "sp", gr_sb[:], grT.rearrange("kc p t -> p kc t")[:, :, t0:t0 + T], [], [Bgr])
            for dc in range(NKC):
                pa, Bpa = proj_fm(woa_s, woa_b, dc, at_sb, Bat)
                sa, Bsa = sg_r.next()
                ACT(P, sa[:], ga_sb[:, dc, :], AF.Sigmoid, [Bga], [Bsa])
                TT(P, "dve", m1[:, dc, :], sa[:], pa[:, 0:T], ALU.mult, [Bsa, Bpa], [Bm1])
            for dc in range(NKC):
                pr, Bpr = proj_fm(wor_s, wor_b, dc, yr_sb, Byr)
                sr, Bsr = sg_r.next()
                ACT(P, sr[:], gr_sb[:, dc, :], AF.Sigmoid, [Bgr], [Bsr])
                tmp, Btmp = ev_r.next()
                TT(P, "dve", tmp[:], sr[:], pr[:, 0:T], ALU.mult, [Bsr, Bpr], [Btmp])
                TT(P, "pool", mg[:, dc, :], m1[:, dc, :], tmp[:], ALU.add, [Bm1, Btmp], [Bmg])
            for dc in range(NKC):
                po, Bpo = proj_fm(wout_s, wout_b, dc, mg, Bmg)
                TT(P, "dve", x_sb[:, dc, :], x_sb[:, dc, :], po[:, 0:T], ALU.add, [Bx, Bpo], [Bx])
            ffn_block("f2")
        if do_a:
            ffn_block("f1")
        DMA(P, "pool", xo.rearrange("kc p t -> p kc t")[:, :, t0:t0 + T], x_sb[:], [Bx], [])
        if do_a:
            rmsnorm(mixg_sb)
            outs_fm = [qTo, kTo, xrTo, grTo, gaTo, ggTo]
            for c in range(NPROJ):
                grp, idx = divmod(c, 8)
                pp, Bpp = proj_fm(win_s, win_b, c, xn, Bxn)
                dst = outs_fm[grp]
                if grp < 2:
                    qf, Bqf = ev_r.next()
                    ACT(P, qf[:], pp[:, 0:T], AF.Identity, [Bpp, Bconst], [Bqf], bias=bfm_sb[:, c:c + 1])
                    s2, Bs2 = evb_r.next()
                    ACT(P, s2[:], qf[:], AF.Square, [Bqf], [Bs2])
                    pn, Bpn = ps_r.next()
                    MM(P, pn[:, 0:T], ones_bf[:], s2[:], True, True, [Bconst, Bs2], [Bpn])
                    sd, Bsd = sd_r.next()
                    ACT(P, sd[:], pn[:, 0:T], AF.Sqrt, [Bpn], [Bsd], bias=eps_sb[:, 0:1], scale=1.0 / HD)
                    rs, Brs = rs_r.next()
                    P.op("dve", lambda e, rs=rs, sd=sd: e.reciprocal(out=rs[:], in_=sd[:]), [Bsd], [Brs])
                    ob, Bob = evb_r.next()
                    gsb = qg_sb if grp == 0 else kg_sb
                    STT(P, "dve", ob[:], qf[:], gsb[:, 0:1], rs[:], ALU.mult, ALU.mult, [Bqf, Brs, Bconst], [Bob])
                    DMA(P, "pool", dst[idx, :, t0:t0 + T], ob[:], [Bob], [])
                else:
                    of, Bof = ev_r.next()
                    ACT(P, of[:], pp[:, 0:T], AF.Identity, [Bpp, Bconst], [Bof], bias=bfm_sb[:, c:c + 1])
                    DMA(P, "pool", dst[idx, :, t0:t0 + T], of[:], [Bof], [])
            DMA(P, "sp", wv_sb[:], wv_s.rearrange("(nh p r) c -> p nh (r c)", nh=2, p=128), [wv_b], [Bwvsb])
            fr, Bfr = f_r.next()
            for tt in range(T // 128):
                for nh in range(2):
                    pv, Bpv = ps_r.next()
                    for kc in range(NKC):
                        MM(P, pv[:, :], xn[:, kc, tt * 128:(tt + 1) * 128], wv_sb[:, nh, kc * 512:(kc + 1) * 512],
                           kc == 0, kc == NKC - 1, [Bxn, Bwvsb], [Bpv])
                    vb, Bvb = vb_r.next()
                    TT(P, "dve", vb[:], pv[:, :], bv_sb[:, nh * 512:(nh + 1) * 512], ALU.add, [Bpv, Bconst], [Bvb])
                    DMA(P, "pool", vo[t0 + tt * 128:t0 + (tt + 1) * 128, nh * 512:(nh + 1) * 512], vb[:], [Bvb], [])
                pf, Bpf = ps_r.next()
                for kc in range(NKC):
                    MM(P, pf[:, 0:8], xn[:, kc, tt * 128:(tt + 1) * 128], wf_b[:, kc * 8:(kc + 1) * 8],
                       kc == 0, kc == NKC - 1, [Bxn, Bconst], [Bpf])
                TT(P, "dve", fr[:, tt, :], pf[:, 0:8], bfl_sb[:], ALU.add, [Bpf, Bconst], [Bfr])
            f2, Bf2 = f2_r.next()
            ACT(P, f2[:], fr[:], AF.Exp, [Bfr], [Bf2], scale=-1.0)
            ACT(P, fr[:], f2[:], AF.Ln, [Bf2], [Bfr], bias=one_sb[:, 0:1])
            TS(P, "dve", f2[:], fr[:], -1.0, None, ALU.mult, None, [Bfr], [Bf2])
            DMA(P, "pool", lfo[t0:t0 + T, :].rearrange("(tt p) n -> p tt n", p=128), f2[:], [Bf2], [])
    P.emit()
    return nc


def tile_fm(W):
    K, N = W.shape
    a = np.ascontiguousarray(W.reshape(K // 128, 128, N // 128, 128).transpose(2, 1, 0, 3))
    return a.reshape(N // 128, 128, K)


def pvec(v):
    return np.ascontiguousarray(v.reshape(-1, 128).T)


def prep_ffn(tag, norm, wg, wu, wd):
    return {
        tag + "_wg": tile_fm(wg).reshape(DFF, 1024),
        tag + "_wu": tile_fm(wu).reshape(DFF, 1024),
        tag + "_wd": tile_fm(wd).reshape(NKC * 128 * 2, NFC * 64),
        tag + "_g": pvec(norm),
    }


FM_COLS = np.concatenate([np.arange(0, 2048), np.arange(3080, 7176)])


def prep_a(inp, l):
    d = prep_ffn("f1", inp["ffn1_norm"][l], inp["ffn1_w_gate"][l], inp["ffn1_w_up"][l], inp["ffn1_w_down"][l])
    W = inp["w_in"][l]
    b = inp["b_in"][l]
    d["win"] = tile_fm(W[:, FM_COLS]).reshape(NPROJ * 128, 1024)
    Wv = W[:, 2048:3072]
    d["wv"] = np.ascontiguousarray(Wv.reshape(8, 128, 2, 512).transpose(2, 1, 0, 3)).reshape(1024, 1024)
    d["wf"] = np.ascontiguousarray(W[:, 3072:3080].reshape(8, 128, 8).transpose(1, 0, 2)).reshape(128, 64)
    d["mixg"] = pvec(inp["mix_norm"][l])
    d["bfm"] = pvec(b[FM_COLS])
    d["bv"] = np.ascontiguousarray(b[2048:3072].reshape(1, 1024))
    d["bfl"] = np.ascontiguousarray(b[3072:3080].reshape(1, 8))
    d["qg"] = np.ascontiguousarray(inp["q_norm"][l].reshape(128, 1))
    d["kg"] = np.ascontiguousarray(inp["k_norm"][l].reshape(128, 1))
    return d


def prep_c(inp, l):
    d = prep_ffn("f2", inp["ffn2_norm"][l], inp["ffn2_w_gate"][l], inp["ffn2_w_up"][l], inp["ffn2_w_down"][l])
    d["woa"] = tile_fm(inp["w_o_attn"][l]).reshape(1024, 1024)
    d["wor"] = tile_fm(inp["w_o_rnn"][l]).reshape(1024, 1024)
    d["wout"] = tile_fm(inp["w_out"][l]).reshape(1024, 1024)
    return d


def build_mix(S, nb=NB, QB=512, CH=1024, do_rnn=True, do_attn=True):
    nc = bass.Bass("TRN2", target_bir_lowering=False)
    P = Prog(nc)
    nJ = S // 128
    nI = S // QB
    sub = QB // 128

    def din(name, shape, dt=F32):
        return nc.dram_tensor(name, list(shape), dt, kind="ExternalInput").ap()

    def dout(name, shape, dt=F32):
        return nc.dram_tensor(name, list(shape), dt, kind="ExternalOutput").ap()

    qT = din("qT", [nb, 128, S], BF16)
    kT = din("kT", [nb, 128, S], BF16)
    vv = din("v", [nb, 128, nJ, 128], BF16)
    lfpm = din("lfpm", [nb, 128, nJ])
    xrT = din("xrT", [128, nb * S])
    grT = din("grT", [128, nb * S])
    convw = din("convw", [128, 4])
    convb = din("convb", [128, 1])
    wa = din("wa", [128, 128])
    wx = din("wx", [128, 128])
    ba = din("ba", [128, 1])
    bx = din("bx", [128, 1])
    lam = din("lam", [128, 1])
    attn_o = dout("attn_o", [nb, S, 128], BF16)
    yr_o = dout("yr_o", [128, nb * S], BF16)

    Bc = P.buf("const")

    def small(name, src, shape):
        t = P.sbuf(name, shape, F32)
        DMA(P, "sp", t[:], src, [], [Bc])
        return t

    one_sb = P.sbuf("one_sb", [128, 1], F32)
    P.op("dve", lambda e: e.memset(one_sb[:], 1.0), [], [Bc])
    ones_bf = P.sbuf("ones_bf", [128, 128], BF16)
    P.op("dve", lambda e: e.memset(ones_bf[:], 1.0), [], [Bc])
    ones_f = P.sbuf("ones_f", [128, 128], F32)
    P.op("dve", lambda e: e.memset(ones_f[:], 1.0), [], [Bc])
    ident_bf = P.sbuf("ident_bf", [128, 128], BF16)
    P.op("pool", lambda e: e.affine_select(out=ident_bf[:], in_=ones_bf[:], pattern=[[1, 128]],
                                           compare_op=ALU.is_equal, fill=0.0, base=0, channel_multiplier=-1),
         [Bc], [Bc])
    tri_f = P.sbuf("tri_f", [128, 128], F32)
    P.op("pool", lambda e: e.affine_select(out=tri_f[:], in_=ones_f[:], pattern=[[1, 128]],
                                           compare_op=ALU.is_ge, fill=0.0, base=0, channel_multiplier=-1),
         [Bc], [Bc])

    st_r = Ring(P, "st", 3, [128, 512], F32, psum=True)
    misc_ps = P.psum("misc_ps", [128, 512], F32)
    Bmisc = P.buf("misc")
    accs = [P.psum(f"acc{s}", [128, 512], F32) for s in range(sub)]
    Bacc = [P.buf(f"acc{s}") for s in range(sub)]

    if do_rnn:
        cw = small("cw", convw, [128, 4])
        cb = small("cb", convb, [128, 1])
        ba_sb = small("ba_sb", ba, [128, 1])
        bx_sb = small("bx_sb", bx, [128, 1])
        lam_sb = small("lam_sb", lam, [128, 1])
        wa_f = small("wa_f", wa, [128, 128])
        wx_f = small("wx_f", wx, [128, 128])
        wa_b = P.sbuf("wa_b", [128, 128], BF16)
        wx_b = P.sbuf("wx_b", [128, 128], BF16)
        P.op("dve", lambda e: e.tensor_copy(out=wa_b[:], in_=wa_f[:]), [Bc], [Bc])
        P.op("dve", lambda e: e.tensor_copy(out=wx_b[:], in_=wx_f[:]), [Bc], [Bc])
        e1 = P.sbuf("e1", [128, 1], F32)
        l1 = P.sbuf("l1", [128, 1], F32)
        cpp = P.sbuf("cpp", [128, 1], F32)
        cpp2 = P.sbuf("cpp2", [128, 1], F32)
        ACT(P, e1[:], lam_sb[:], AF.Exp, [Bc], [Bc], scale=-1.0)
        ACT(P, l1[:], e1[:], AF.Ln, [Bc], [Bc], bias=one_sb[:, 0:1])
        TS(P, "dve", cpp[:], l1[:], -8.0, None, ALU.mult, None, [Bc], [Bc])
        TS(P, "dve", cpp2[:], l1[:], -16.0, None, ALU.mult, None, [Bc], [Bc])

        xin_r = Ring(P, "xin", 2, [128, CH + 3], F32)
        gin_r = Ring(P, "gin", 2, [128, CH], F32)
        y_r = Ring(P, "y", 2, [128, CH], F32)
        yb_r = Ring(P, "yb", 2, [128, CH], BF16)
        r_r = Ring(P, "r", 2, [128, CH], F32)
        i_r = Ring(P, "i", 2, [128, CH], F32)
        a_r = Ring(P, "a", 2, [128, CH], F32)
        a2_r = Ring(P, "a2", 2, [128, CH], F32)
        u_r = Ring(P, "u", 2, [128, CH], F32)
        h_r = Ring(P, "h", 2, [128, CH], F32)
        t_r = Ring(P, "t", 2, [128, CH], F32)
        yo_r = Ring(P, "yo", 2, [128, CH], BF16)
        hp_r = Ring(P, "hp", 2, [128, 1], F32)
        nch = S // CH
        prev_xin = None
        hp = None
        for b in range(nb):
            for ci in range(nch):
                off = b * S + ci * CH
                xin, Bxin = xin_r.next()
                DMA(P, "sp", xin[:, 3:3 + CH], xrT[:, off:off + CH], [], [Bxin])
                if ci == 0:
                    P.op("pool", lambda e, xin=xin: e.memset(xin[:, 0:3], 0.0), [], [Bxin])
                else:
                    pxin, Bpxin = prev_xin
                    P.op("pool", lambda e, xin=xin, pxin=pxin: e.tensor_copy(out=xin[:, 0:3], in_=pxin[:, CH:CH + 3]),
                         [Bpxin], [Bxin])
                prev_xin = (xin, Bxin)
                gin, Bgin = gin_r.next()
                DMA(P, "sp", gin[:], grT[:, off:off + CH], [], [Bgin])
                y, By = y_r.next()
                TS(P, "dve", y[:], xin[:, 3:3 + CH], cw[:, 3:4], cb[:, 0:1], ALU.mult, ALU.add, [Bxin, Bc], [By])
                for tap in (2, 1, 0):
                    STT(P, "dve", y[:], xin[:, tap:tap + CH], cw[:, tap:tap + 1], y[:], ALU.mult, ALU.add,
                        [Bxin, By, Bc], [By])
                yb, Byb = yb_r.next()
                P.op("pool", lambda e, yb=yb, y=y: e.tensor_copy(out=yb[:], in_=y[:]), [By], [Byb])
                r, Br = r_r.next()
                ig, Bi = i_r.next()
                for qq in range(CH // 512):
                    sl = slice(qq * 512, (qq + 1) * 512)
                    pa, Bpa = st_r.next()
                    MM(P, pa[:, :], wa_b[:], yb[:, sl], True, True, [Bc, Byb], [Bpa])
                    ACT(P, r[:, sl], pa[:, :], AF.Sigmoid, [Bpa, Bc], [Br], bias=ba_sb[:, 0:1])
                    px, Bpx = st_r.next()
                    MM(P, px[:, :], wx_b[:], yb[:, sl], True, True, [Bc, Byb], [Bpx])
                    ACT(P, ig[:, sl], px[:, :], AF.Sigmoid, [Bpx, Bc], [Bi], bias=bx_sb[:, 0:1])
                t, Bt = t_r.next()
                TT(P, "pool", t[:], gin[:], gin[:], ALU.mult, [Bgin], [Bt])
                TS(P, "pool", t[:], t[:], 0.044715, 1.0, ALU.mult, ALU.add, [Bt], [Bt])
                TT(P, "pool", t[:], t[:], gin[:], ALU.mult, [Bt, Bgin], [Bt])
                ACT(P, t[:], t[:], AF.Sigmoid, [Bt], [Bt], scale=1.5957691216057308)
                TT(P, "pool", t[:], t[:], gin[:], ALU.mult, [Bt, Bgin], [Bt])
                a, Ba = a_r.next()
                ACT(P, a[:], r[:], AF.Exp, [Br, Bc], [Ba], scale=cpp[:, 0:1])
                a2, Ba2 = a2_r.next()
                ACT(P, a2[:], r[:], AF.Exp, [Br, Bc], [Ba2], scale=cpp2[:, 0:1])
                TS(P, "dve", a2[:], a2[:], -1.0, 1.0, ALU.mult, ALU.add, [Ba2], [Ba2])
                ACT(P, a2[:], a2[:], AF.Ln, [Ba2], [Ba2])
                ACT(P, a2[:], a2[:], AF.Exp, [Ba2], [Ba2], scale=0.5)
                u, Bu = u_r.next()
                TT(P, "dve", u[:], ig[:], y[:], ALU.mult, [Bi, By], [Bu])
                TT(P, "dve", u[:], u[:], a2[:], ALU.mult, [Bu, Ba2], [Bu])
                h, Bh = h_r.next()
                if ci == 0:
                    P.op("dve", lambda e, h=h, a=a, u=u: e.tensor_tensor_scan(
                        out=h[:], data0=a[:], data1=u[:], initial=0.0, op0=ALU.mult, op1=ALU.add),
                        [Ba, Bu], [Bh])
                else:
                    hpt, Bhp = hp
                    P.op("dve", lambda e, h=h, a=a, u=u, hpt=hpt: e.tensor_tensor_scan(
                        out=h[:], data0=a[:], data1=u[:], initial=hpt[:, 0:1], op0=ALU.mult, op1=ALU.add),
                        [Ba, Bu, Bhp], [Bh])
                hpt, Bhp = hp_r.next()
                P.op("dve", lambda e, h=h, hpt=hpt: e.tensor_copy(out=hpt[:], in_=h[:, CH - 1:CH]), [Bh], [Bhp])
                hp = (hpt, Bhp)
                yo, Byo = yo_r.next()
                TT(P, "dve", yo[:], h[:], t[:], ALU.mult, [Bh, Bt], [Byo])
                DMA(P, "pool", yr_o[:, off:off + CH], yo[:], [Byo], [])

    if do_attn:
        q_sb = P.sbuf("q_sb", [128, S], BF16); Bq = P.buf("q")
        k_sb = P.sbuf("k_sb", [128, S], BF16); Bk = P.buf("k")
        v_sb = P.sbuf("v_sb", [128, nJ, 129], BF16); Bv = P.buf("v")
        lf_sb = P.sbuf("lf_sb", [128, nJ], F32); Blf = P.buf("lf")
        tot = P.sbuf("tot", [128, nJ], F32); Btot = P.buf("tot")
        incl = P.sbuf("incl", [128, nJ], F32); Bincl = P.buf("incl")
        excl = P.sbuf("excl", [128, nJ], F32); Bexcl = P.buf("excl")
        c_pm = P.sbuf("c_pm", [128, nJ], F32); Bcpm = P.buf("cpm")
        bias_r = Ring(P, "biasI", 2, [128, nJ], F32)
        dg_r = Ring(P, "dg", 2, [128, QB], BF16)
        pt_r = Ring(P, "pt", 4, [128, QB], BF16)
        rc_r = Ring(P, "rc", 4, [128, 1], F32)
        ob_r = Ring(P, "ob", 4, [128, 128], BF16)
        P.op("dve", lambda e: e.memset(v_sb[:, :, 128:129], 1.0), [], [Bv])
        for b in range(nb):
            DMA(P, "sp", q_sb[:], qT[b], [], [Bq])
            DMA(P, "sp", k_sb[:], kT[b], [], [Bk])
            DMA(P, "sp", v_sb[:, :, 0:128], vv[b], [], [Bv])
            DMA(P, "sp", lf_sb[:], lfpm[b], [], [Blf])
            MM(P, misc_ps[:, 0:nJ], tri_f[:], lf_sb[:], True, True, [Bc, Blf], [Bmisc])
            p2, Bp2 = st_r.next()
            MM(P, p2[:, 0:nJ], ones_f[:], lf_sb[:], True, True, [Bc, Blf], [Bp2])
            P.op("dve", lambda e, p2=p2: e.tensor_copy(out=tot[:], in_=p2[:, 0:nJ]), [Bp2], [Btot])
            P.op("dve", lambda e: e.tensor_tensor_scan(out=incl[:], data0=ones_f[:, 0:nJ], data1=tot[:], initial=0.0,
                                                       op0=ALU.mult, op1=ALU.add), [Btot, Bc], [Bincl])
            TT(P, "dve", excl[:], incl[:], tot[:], ALU.subtract, [Bincl, Btot], [Bexcl])
            TT(P, "dve", c_pm[:], excl[:], misc_ps[:, 0:nJ], ALU.add, [Bexcl, Bmisc], [Bcpm])

            steps = [(I, J) for I in range(nI) for J in range(sub * I + sub)]
            blk = {}
            stq = {}

            def emit_qk(n):
                I, J = steps[n]
                if I not in blk:
                    nJI = sub * I + sub
                    bi, Bbi = bias_r.next()
                    TS(P, "dve", bi[:, 0:nJI], c_pm[:, 0:nJI], -1.0, excl[:, sub * I:sub * I + 1], ALU.mult, ALU.add,
                       [Bcpm, Bexcl], [Bbi])
                    dg, Bdg = dg_r.next()
                    for jj in range(sub):
                        TS(P, "dve", dg[:, jj * 128:(jj + 1) * 128], ident_bf[:], bi[:, sub * I + jj:sub * I + jj + 1],
                           -1.0, ALU.mult, ALU.mult, [Bc, Bbi], [Bdg])
                    blk[I] = (bi, Bbi, dg, Bdg)
                bi, Bbi, dg, Bdg = blk[I]
                d = J - sub * I
                c0 = max(d, 0) * 128
                st, Bst = st_r.next()
                MM(P, st[:, c0:QB], ones_bf[:], dg[:, c0:QB], True, False, [Bc, Bdg], [Bst])
                MM(P, st[:, c0:QB], k_sb[:, J * 128:(J + 1) * 128], q_sb[:, I * QB + c0:(I + 1) * QB], False, True,
                   [Bk, Bq], [Bst])
                stq[n] = (st, Bst)

            LOOK = 2
            for n in range(min(LOOK, len(steps))):
                emit_qk(n)
            for n in range(len(steps)):
                if n + LOOK < len(steps):
                    emit_qk(n + LOOK)
                I, J = steps[n]
                bi, Bbi, dg, Bdg = blk[I]
                st, Bst = stq.pop(n)
                d = J - sub * I
                c0 = max(d, 0) * 128
                pT, BpT = pt_r.next()
                ACT(P, pT[:, c0:QB], st[:, c0:QB], AF.Exp, [Bst, Bbi], [BpT], bias=bi[:, J:J + 1])
                if d >= 0:
                    P.op("pool", lambda e, pT=pT, c0=c0: e.affine_select(
                        out=pT[:, c0:c0 + 128], in_=pT[:, c0:c0 + 128], pattern=[[1, 128]],
                        compare_op=ALU.is_ge, fill=0.0, base=0, channel_multiplier=-1), [BpT], [BpT])
                for s in range(max(d, 0), sub):
                    last = (J == sub * I + s)
                    MM(P, accs[s][:, 0:129], pT[:, s * 128:(s + 1) * 128], v_sb[:, J, :], J == 0, last,
                       [BpT, Bv], [Bacc[s]])
                    if last:
                        rc, Brc = rc_r.next()
                        P.op("dve", lambda e, rc=rc, s=s: e.reciprocal(out=rc[:], in_=accs[s][:, 128:129]),
                             [Bacc[s]], [Brc])
                        ob, Bob = ob_r.next()
                        TS(P, "dve", ob[:], accs[s][:, 0:128], rc[:, 0:1], None, ALU.mult, None, [Bacc[s], Brc], [Bob])
                        tq = (I * sub + s) * 128
                        DMA(P, "pool", attn_o[b, tq:tq + 128, :], ob[:], [Bob], [])
    P.emit()
    return nc


NTOK = SEQ * NB // NCORE
TPB = SEQ // NTOK
_PROGS = {}


def _prog(key):
    if key not in _PROGS:
        if key == "A":
            _PROGS[key] = build_tok(NTOK, False, True)
        elif key == "CA":
            _PROGS[key] = build_tok(NTOK, True, True)
        elif key == "C":
            _PROGS[key] = build_tok(NTOK, True, False)
        elif key == "MIX":
            _PROGS[key] = build_mix(SEQ, NB)
    return _PROGS[key]


def _launch(key, in_maps):
    res = run_bass_kernel_spmd(_prog(key), in_maps, core_ids=list(range(NCORE)))
    return res.results


def _bd(w2):
    m = np.zeros((128, 128), np.float32)
    m[:64, :64] = w2[0]
    m[64:, 64:] = w2[1]
    return m


def _mix_inputs(inp, l, tok_out):
    maps = []
    vfull = [np.concatenate([tok_out[b * TPB + tc]["vo"] for tc in range(TPB)], axis=0) for b in range(NB)]
    lffull = [np.concatenate([tok_out[b * TPB + tc]["lfo"] for tc in range(TPB)], axis=0) for b in range(NB)]
    for h in range(NCORE):
        cs = slice(h * 128, (h + 1) * 128)
        qT = np.stack([np.concatenate([tok_out[b * TPB + tc]["qTo"][h] for tc in range(TPB)], axis=1) for b in range(NB)])
        kT = np.stack([np.concatenate([tok_out[b * TPB + tc]["kTo"][h] for tc in range(TPB)], axis=1) for b in range(NB)])
        v = np.stack([vfull[b][:, cs].reshape(SEQ // 128, 128, 128).transpose(1, 0, 2) for b in range(NB)])
        lfpm = np.stack([lffull[b][:, h].reshape(SEQ // 128, 128).T for b in range(NB)])
        xrT = np.concatenate([tok_out[b * TPB + tc]["xrTo"][h] for b in range(NB) for tc in range(TPB)], axis=1)
        grT = np.concatenate([tok_out[b * TPB + tc]["grTo"][h] for b in range(NB) for tc in range(TPB)], axis=1)
        maps.append(dict(
            qT=np.ascontiguousarray(qT), kT=np.ascontiguousarray(kT), v=np.ascontiguousarray(v),
            lfpm=np.ascontiguousarray(lfpm), xrT=np.ascontiguousarray(xrT), grT=np.ascontiguousarray(grT),
            convw=np.ascontiguousarray(inp["conv_w"][l][:, cs].T),
            convb=np.ascontiguousarray(inp["conv_b"][l][cs].reshape(128, 1)),
            wa=_bd(inp["lru_w_a"][l][2 * h:2 * h + 2]), wx=_bd(inp["lru_w_x"][l][2 * h:2 * h + 2]),
            ba=np.ascontiguousarray(inp["lru_b_a"][l][cs].reshape(128, 1)),
            bx=np.ascontiguousarray(inp["lru_b_x"][l][cs].reshape(128, 1)),
            lam=np.ascontiguousarray(inp["lru_lambda"][l][cs].reshape(128, 1)),
        ))
    return maps


def _c_inputs(tok_out, mix_out):
    maps = []
    for c in range(NCORE):
        b, tc = divmod(c, TPB)
        ts = slice(tc * NTOK, (tc + 1) * NTOK)
        attnT = np.stack([mix_out[h]["attn_o"][b, ts, :].T for h in range(NH)])
        yrT = np.stack([mix_out[kc]["yr_o"][:, b * SEQ + tc * NTOK:b * SEQ + (tc + 1) * NTOK] for kc in range(NKC)])
        maps.append(dict(
            xT=tok_out[c]["xo"], attnT=np.ascontiguousarray(attnT), yrT=np.ascontiguousarray(yrT),
            gaT=tok_out[c]["gaTo"], grT=tok_out[c]["ggTo"]))
    return maps


def kernel(**inp):
    inp = {k: np.asarray(v) for k, v in inp.items()}
    x = inp["x"]
    cur = []
    for c in range(NCORE):
        b, tc = divmod(c, TPB)
        cur.append(dict(xT=np.ascontiguousarray(x[b, tc * NTOK:(tc + 1) * NTOK, :].T).reshape(NKC, 128, NTOK)))
    tok_out = None
    for l in range(DEPTH):
        wa_ = prep_a(inp, l)
        if l == 0:
            maps = [dict(cur[c], **wa_) for c in range(NCORE)]
            tok_out = _launch("A", maps)
        else:
            wc_ = prep_c(inp, l - 1)
            maps = [dict(cur[c], **wc_, **wa_) for c in range(NCORE)]
            tok_out = _launch("CA", maps)
        mix_out = _launch("MIX", _mix_inputs(inp, l, tok_out))
        cur = _c_inputs(tok_out, mix_out)
    wc_ = prep_c(inp, DEPTH - 1)
    fin = _launch("C", [dict(cur[c], **wc_) for c in range(NCORE)])
    out = np.empty((NB, SEQ, D), np.float32)
    for c in range(NCORE):
        b, tc = divmod(c, TPB)
        out[b, tc * NTOK:(tc + 1) * NTOK, :] = fin[c]["xo"].reshape(D, NTOK).T
    return out
```
